# Optimizing a Trainium2 kernel written in Bass

```python
import math
import jax, jax.numpy as jnp
from jax import lax
import numpy as np

D_MODEL = 1024
BATCH = 32
SEQ = 2048
DEPTH = 2

GRID_W = 64
CTX_LEN = 256
CONV_W = 3
M_HEADS = 4
M_HEAD_DIM = 128
M_WIDTH = M_HEADS * M_HEAD_DIM
M_CHUNK = 64
A_HEADS = 4
A_SUB_DIM = 64
A_V_DIM = 2 * A_SUB_DIM
A_WIDTH = A_HEADS * A_V_DIM
Q_BLOCK = 128
ROPE_BASE = 10000.0
ROPE_FREQS = A_SUB_DIM // 4
S_HEADS = 8
S_HEAD_DIM = 64
S_WIDTH = S_HEADS * S_HEAD_DIM
S_GROUPS = 2
S_STATE = 128
S_CHUNK = 64
S_CONV_DIM = S_WIDTH + 2 * S_GROUPS * S_STATE
D_FF = 2816
N_BRANCH = 3
N_MOD = 6
IN_SPLITS = (2 * M_WIDTH, M_WIDTH, M_WIDTH, 4 * M_HEADS, A_WIDTH, A_WIDTH, A_WIDTH, S_WIDTH, S_CONV_DIM, 2 * S_HEADS, N_BRANCH * D_MODEL)
D_IN = sum(IN_SPLITS)
NORM_EPS = 1e-6

kernel_name = "hybrid_mlstm_diffattn_ssd_convffn_prefix"


def rmsnorm(x, g):
    xf = x.astype(jnp.float32)
    y = xf * lax.rsqrt(jnp.mean(xf * xf, axis=-1, keepdims=True) + NORM_EPS)
    return (y * g.astype(jnp.float32)).astype(x.dtype)


def modulate(h, shift, scale):
    return h * (1.0 + scale) + shift


def dwconv(x, w, b=None):
    k = w.shape[0]
    y = lax.conv_general_dilated(x, w[:, None, :].astype(x.dtype), window_strides=(1,),
                                 padding=[(k // 2, k // 2)], dimension_numbers=('NWC', 'WIO', 'NWC'),
                                 feature_group_count=x.shape[-1])
    if b is not None:
        y = y + b.astype(x.dtype)
    return y


def to_chunks(a, axis, size):
    shp = a.shape
    a = a.reshape(shp[:axis] + (shp[axis] // size, size) + shp[axis + 1:])
    return jnp.moveaxis(a, axis, 0)


def from_chunks(a, axis):
    a = jnp.moveaxis(a, 0, axis)
    shp = a.shape
    return a.reshape(shp[:axis] + (shp[axis] * shp[axis + 1],) + shp[axis + 2:])


def rev(a, axis, on):
    return jnp.flip(a, axis=axis) if on else a


def mlstm_scan(q, k, v, ig, lf, state, emit):
    L = M_CHUNK
    causal = jnp.tril(jnp.ones((L, L), dtype=bool))

    def step(carry, inp):
        C, n, m = carry
        qc, kc, vc, ic, fc = inp
        b = jnp.cumsum(fc, axis=-1)
        g_prev = b + m[..., None]
        d_mat = jnp.where(causal, b[..., :, None] - b[..., None, :] + ic[..., None, :], -jnp.inf)
        m_t = jnp.maximum(g_prev, jnp.max(d_mat, axis=-1))
        m_new = m_t[..., -1]
        w_s = jnp.exp(b[..., -1:] - b + ic - m_new[..., None])
        decay = jnp.exp(b[..., -1] + m - m_new)
        C_new = decay[..., None, None] * C + jnp.einsum('bhs,bhsk,bhsv->bhkv', w_s, kc, vc)
        n_new = decay[..., None] * n + jnp.einsum('bhs,bhsk->bhk', w_s, kc)
        if not emit:
            return (C_new, n_new, m_new), None
        w_prev = jnp.exp(g_prev - m_t)
        w = jnp.exp(d_mat - m_t[..., None]) * jnp.einsum('bhtk,bhsk->bhts', qc, kc)
        num = w_prev[..., None] * jnp.einsum('bhtk,bhkv->bhtv', qc, C) + jnp.einsum('bhts,bhsv->bhtv', w, vc)
        den = w_prev * jnp.einsum('bhtk,bhk->bht', qc, n) + jnp.sum(w, axis=-1)
        h = num / jnp.maximum(jnp.abs(den), jnp.exp(-m_t))[..., None]
        return (C_new, n_new, m_new), h

    xs = (to_chunks(q, 2, L), to_chunks(k, 2, L), to_chunks(v, 2, L), to_chunks(ig, 2, L), to_chunks(lf, 2, L))
    state, h = lax.scan(step, state, xs)
    return (from_chunks(h, 2) if emit else None), state


def mlstm_prep(qk_pre, v_pre, gate_pre, conv_w, conv_b, ig_b, fg_b):
    bsz, t = v_pre.shape[:2]
    qk = jax.nn.silu(dwconv(qk_pre, conv_w, conv_b)).astype(jnp.float32)
    q, k = jnp.split(qk, 2, axis=-1)

    def heads(a):
        return a.astype(jnp.float32).reshape(bsz, t, M_HEADS, M_HEAD_DIM).transpose(0, 2, 1, 3)

    g = gate_pre.astype(jnp.float32).reshape(bsz, t, 2, 2, M_HEADS)
    ig = (g[:, :, :, 0] + ig_b).transpose(2, 0, 3, 1)
    lf = jax.nn.log_sigmoid(g[:, :, :, 1] + fg_b).transpose(2, 0, 3, 1)
    return heads(q), heads(k) * M_HEAD_DIM ** -0.5, heads(v_pre), ig, lf


def mlstm_out(h, o_pre, norm_g):
    bsz, t = o_pre.shape[:2]
    h = rmsnorm(h.transpose(0, 2, 1, 3), norm_g.reshape(M_HEADS, M_HEAD_DIM)).reshape(bsz, t, M_WIDTH)
    return (jax.nn.sigmoid(o_pre.astype(jnp.float32)) * h).astype(o_pre.dtype)


def mlstm_mixer(lat, ctx, conv_w, conv_b, ig_b, fg_b, norm_g, emit_ctx):
    q, k, v, ig, lf = mlstm_prep(lat[0], lat[1], lat[3], conv_w, conv_b, ig_b, fg_b)
    qc, kc, vc, igc, lfc = mlstm_prep(ctx[0], ctx[1], ctx[3], conv_w, conv_b, ig_b, fg_b)
    bsz = q.shape[0]
    zeros = (jnp.zeros((bsz, M_HEADS, M_HEAD_DIM, M_HEAD_DIM), jnp.float32),
             jnp.zeros((bsz, M_HEADS, M_HEAD_DIM), jnp.float32),
             jnp.zeros((bsz, M_HEADS), jnp.float32))
    h_lat, h_ctx = 0.0, 0.0
    for d in range(2):
        hc, st = mlstm_scan(rev(qc, 2, d), rev(kc, 2, d), rev(vc, 2, d), rev(igc[d], 2, d), rev(lfc[d], 2, d), zeros, emit_ctx)
        hl, _ = mlstm_scan(rev(q, 2, d), rev(k, 2, d), rev(v, 2, d), rev(ig[d], 2, d), rev(lf[d], 2, d), st, True)
        h_lat = h_lat + rev(hl, 2, d)
        if emit_ctx:
            h_ctx = h_ctx + rev(hc, 2, d)
    out_l = mlstm_out(h_lat, lat[2], norm_g)
    out_c = mlstm_out(h_ctx, ctx[2], norm_g) if emit_ctx else None
    return out_l, out_c


def rope_2d(x, cos, sin):
    shp = x.shape
    xr = x.reshape(shp[:-1] + (2, 2, ROPE_FREQS))
    x1, x2 = xr[..., 0, :], xr[..., 1, :]
    cb, sb = cos[None, :, None, None], sin[None, :, None, None]
    return jnp.stack([x1 * cb - x2 * sb, x2 * cb + x1 * sb], axis=-2).reshape(shp)


def diff_attention(q, k, v, qc, kc, vc, qn_g, kn_g, lam_p, subln_g, lam_init, cos, sin, emit_ctx):
    def heads_qk(a, g, rot):
        bsz, t = a.shape[:2]
        a = rmsnorm(a.reshape(bsz, t, A_HEADS, 2, A_SUB_DIM), g).astype(jnp.float32)
        if rot:
            a = rope_2d(a, cos, sin)
        return a.transpose(0, 2, 3, 1, 4)

    def heads_v(a):
        bsz, t = a.shape[:2]
        return a.reshape(bsz, t, A_HEADS, A_V_DIM).transpose(0, 2, 1, 3).astype(jnp.float32)

    lam_p = lam_p.astype(jnp.float32)
    lam = jnp.exp(jnp.sum(lam_p[0] * lam_p[1])) - jnp.exp(jnp.sum(lam_p[2] * lam_p[3])) + lam_init
    scale = A_SUB_DIM ** -0.5

    def attend(qb, kk, vv):
        s = jnp.einsum('bhjqd,bhjkd->bhjqk', qb, kk) * scale
        p = jax.nn.softmax(s, axis=-1)
        return jnp.einsum('bhqk,bhkv->bhqv', p[:, :, 0] - lam * p[:, :, 1], vv)

    k_ctx, v_ctx = heads_qk(kc, kn_g, False), heads_v(vc)
    k_all = jnp.concatenate([heads_qk(k, kn_g, True), k_ctx], axis=3)
    v_all = jnp.concatenate([heads_v(v), v_ctx], axis=2)
    q_blocks = to_chunks(heads_qk(q, qn_g, True), 3, Q_BLOCK)
    o_lat = from_chunks(lax.map(lambda qb: attend(qb, k_all, v_all), q_blocks), 2)

    def post(o):
        o = rmsnorm(o.transpose(0, 2, 1, 3), subln_g) * (1.0 - lam_init)
        return o.reshape(o.shape[0], o.shape[1], A_WIDTH).astype(q.dtype)

    out_c = post(attend(heads_qk(qc, qn_g, False), k_ctx, v_ctx)) if emit_ctx else None
    return post(o_lat), out_c


def ssd_scan(x, dt, a_coef, bm, cm, state, emit):
    L = S_CHUNK
    rep = S_HEADS // S_GROUPS
    causal = jnp.tril(jnp.ones((L, L), dtype=bool))[None, :, :, None]

    def step(h, inp):
        xk, dtk, bk, ck = inp
        s = jnp.cumsum(dtk * a_coef, axis=1)
        bh = jnp.repeat(bk, rep, axis=2)
        xdt = xk * dtk[..., None]
        h_new = (jnp.exp(s[:, -1])[..., None, None] * h
                 + jnp.einsum('bsh,bshp,bshn->bhpn', jnp.exp(s[:, -1:] - s), xdt, bh))
        if not emit:
            return h_new, None
        ch = jnp.repeat(ck, rep, axis=2)
        seg = jnp.exp(jnp.where(causal, s[:, :, None] - s[:, None], -jnp.inf))
        y = (jnp.einsum('btsh,bshp->bthp', jnp.einsum('bthn,bshn->btsh', ch, bh) * seg, xdt)
             + jnp.exp(s)[..., None] * jnp.einsum('bthn,bhpn->bthp', ch, h))
        return h_new, y

    xs = (to_chunks(x, 1, L), to_chunks(dt, 1, L), to_chunks(bm, 1, L), to_chunks(cm, 1, L))
    state, y = lax.scan(step, state, xs)
    return (from_chunks(y, 1) if emit else None), state


def ssd_prep(xbc_pre, dt_pre, conv_w, conv_b, dt_bias):
    bsz, t = xbc_pre.shape[:2]
    xbc = jax.nn.silu(dwconv(xbc_pre, conv_w, conv_b)).astype(jnp.float32)
    xs, bm, cm = jnp.split(xbc, [S_WIDTH, S_WIDTH + S_GROUPS * S_STATE], axis=-1)
    dt = jax.nn.softplus(dt_pre.astype(jnp.float32).reshape(bsz, t, 2, S_HEADS) + dt_bias)
    return (xs.reshape(bsz, t, S_HEADS, S_HEAD_DIM), bm.reshape(bsz, t, S_GROUPS, S_STATE),
            cm.reshape(bsz, t, S_GROUPS, S_STATE), dt)


def ssd_out(y, xs, z, d_skip, norm_g):
    bsz, t = z.shape[:2]
    y = (y + d_skip.astype(jnp.float32)[:, None] * xs).reshape(bsz, t, S_WIDTH) * jax.nn.silu(z.astype(jnp.float32))
    y = rmsnorm(y.reshape(bsz, t, S_GROUPS, S_WIDTH // S_GROUPS), norm_g.reshape(S_GROUPS, S_WIDTH // S_GROUPS))
    return y.reshape(bsz, t, S_WIDTH).astype(z.dtype)


def ssd_mixer(z, xbc_pre, dt_pre, zc, xbcc_pre, dtc_pre, conv_w, conv_b, dt_bias, a_log, d_skip, norm_g, emit_ctx):
    a_coef = -jnp.exp(a_log.astype(jnp.float32))
    x, bm, cm, dt = ssd_prep(xbc_pre, dt_pre, conv_w, conv_b, dt_bias)
    xc, bc, cc, dtc = ssd_prep(xbcc_pre, dtc_pre, conv_w, conv_b, dt_bias)
    zeros = jnp.zeros((x.shape[0], S_HEADS, S_HEAD_DIM, S_STATE), jnp.float32)
    y_lat, y_ctx = 0.0, 0.0
    for d in range(2):
        yc, st = ssd_scan(rev(xc, 1, d), rev(dtc[:, :, d], 1, d), a_coef[d], rev(bc, 1, d), rev(cc, 1, d), zeros, emit_ctx)
        yl, _ = ssd_scan(rev(x, 1, d), rev(dt[:, :, d], 1, d), a_coef[d], rev(bm, 1, d), rev(cm, 1, d), st, True)
        y_lat = y_lat + rev(yl, 1, d)
        if emit_ctx:
            y_ctx = y_ctx + rev(yc, 1, d)
    out_l = ssd_out(y_lat, x, z, d_skip, norm_g)
    out_c = ssd_out(y_ctx, xc, zc, d_skip, norm_g) if emit_ctx else None
    return out_l, out_c


def gated_merge(y_m, y_a, y_s, gate_pre, w_m, w_a, w_s, w_o):
    g_m, g_a, g_s = jnp.split(jax.nn.sigmoid(gate_pre), N_BRANCH, axis=-1)
    return (g_m * (y_m @ w_m) + g_a * (y_a @ w_a) + g_s * (y_s @ w_s)) @ w_o


def conv_ffn(h, w_up, conv_w, w_down):
    a, g = jnp.split(dwconv(h @ w_up, conv_w), 2, axis=-1)
    return (jax.nn.silu(g) * a) @ w_down


def setup_inputs(seed: int = 0) -> dict:
    key = jax.random.key(seed)
    ks = jax.random.split(key, 32)
    f32 = jnp.float32

    def nrm(i, shape, scale):
        return scale * jax.random.normal(ks[i], shape, f32)

    def gain(i, shape):
        return 1.0 + nrm(i, shape, 0.05)

    dt0 = jnp.exp(jax.random.uniform(ks[20], (DEPTH, 2, S_HEADS), f32, math.log(1e-3), math.log(1e-1)))
    return {
        "x": nrm(0, (BATCH, SEQ, D_MODEL), 1.0),
        "c": nrm(1, (BATCH, D_MODEL), 1.0),
        "ctx": nrm(2, (BATCH, CTX_LEN, D_MODEL), 1.0),
        "c_ctx": nrm(3, (D_MODEL,), 1.0),
        "w_mod": nrm(4, (DEPTH, D_MODEL, N_MOD * D_MODEL), 0.5 * D_MODEL ** -0.5),
        "b_mod": nrm(5, (DEPTH, N_MOD * D_MODEL), 0.02),
        "norm1_g": gain(6, (DEPTH, D_MODEL)),
        "norm2_g": gain(7, (DEPTH, D_MODEL)),
        "w_in": nrm(8, (DEPTH, D_MODEL, D_IN), D_MODEL ** -0.5),
        "m_conv_w": nrm(9, (DEPTH, CONV_W, 2 * M_WIDTH), CONV_W ** -0.5),
        "m_conv_b": nrm(10, (DEPTH, 2 * M_WIDTH), 0.02),
        "m_igate_b": nrm(11, (DEPTH, 2, M_HEADS), 0.1),
        "m_fgate_b": jnp.linspace(3.0, 6.0, M_HEADS, dtype=f32) + nrm(12, (DEPTH, 2, M_HEADS), 0.1),
        "m_norm_g": gain(13, (DEPTH, M_WIDTH)),
        "a_qnorm_g": gain(14, (DEPTH, A_SUB_DIM)),
        "a_knorm_g": gain(15, (DEPTH, A_SUB_DIM)),
        "a_lambda": nrm(16, (DEPTH, 4, A_SUB_DIM), 0.1),
        "a_subln_g": gain(17, (DEPTH, A_V_DIM)),
        "s_conv_w": nrm(18, (DEPTH, CONV_W, S_CONV_DIM), CONV_W ** -0.5),
        "s_conv_b": nrm(19, (DEPTH, S_CONV_DIM), 0.02),
        "s_dt_bias": dt0 + jnp.log(-jnp.expm1(-dt0)),
        "s_a_log": jnp.log(jax.random.uniform(ks[21], (DEPTH, 2, S_HEADS), f32, 1.0, 16.0)),
        "s_d": gain(22, (DEPTH, S_HEADS)),
        "s_norm_g": gain(23, (DEPTH, S_WIDTH)),
        "w_branch_m": nrm(24, (DEPTH, M_WIDTH, D_MODEL), M_WIDTH ** -0.5),
        "w_branch_a": nrm(25, (DEPTH, A_WIDTH, D_MODEL), A_WIDTH ** -0.5),
        "w_branch_s": nrm(26, (DEPTH, S_WIDTH, D_MODEL), S_WIDTH ** -0.5),
        "w_out": nrm(27, (DEPTH, D_MODEL, D_MODEL), D_MODEL ** -0.5),
        "w_up": nrm(28, (DEPTH, D_MODEL, 2 * D_FF), D_MODEL ** -0.5),
        "ffn_conv_w": nrm(29, (DEPTH, CONV_W, 2 * D_FF), CONV_W ** -0.5),
        "w_down": nrm(30, (DEPTH, D_FF, D_MODEL), D_FF ** -0.5),
    }


def reference(x, c, ctx, c_ctx, w_mod, b_mod, norm1_g, norm2_g, w_in, m_conv_w, m_conv_b, m_igate_b, m_fgate_b,
              m_norm_g, a_qnorm_g, a_knorm_g, a_lambda, a_subln_g, s_conv_w, s_conv_b, s_dt_bias, s_a_log, s_d,
              s_norm_g, w_branch_m, w_branch_a, w_branch_s, w_out, w_up, ffn_conv_w, w_down):
    t = x.shape[1]
    rows = t // GRID_W
    row = jnp.repeat(jnp.arange(rows), GRID_W)
    col = jnp.tile(jnp.arange(GRID_W), rows)
    inv_freq = ROPE_BASE ** (-jnp.arange(ROPE_FREQS, dtype=jnp.float32) / ROPE_FREQS)
    ang = jnp.stack([row, col], axis=-1).astype(jnp.float32)[..., None] * inv_freq
    cos, sin = jnp.cos(ang), jnp.sin(ang)
    split_pts = [int(p) for p in np.cumsum(IN_SPLITS)[:-1]]

    xc = ctx
    for l in range(DEPTH):
        emit = l < DEPTH - 1
        lam_init = 0.8 - 0.6 * math.exp(-0.3 * l)
        mod = jnp.split(jax.nn.silu(c) @ w_mod[l] + b_mod[l], N_MOD, axis=-1)
        modc = jnp.split(jax.nn.silu(c_ctx) @ w_mod[l] + b_mod[l], N_MOD, axis=-1)

        h = modulate(rmsnorm(x, norm1_g[l]), mod[0][:, None], mod[1][:, None])
        hc = modulate(rmsnorm(xc, norm1_g[l]), modc[0], modc[1])
        u = jnp.split(h @ w_in[l], split_pts, axis=-1)
        uc = jnp.split(hc @ w_in[l], split_pts, axis=-1)
        y_m, y_mc = mlstm_mixer(u[0:4], uc[0:4], m_conv_w[l], m_conv_b[l], m_igate_b[l], m_fgate_b[l], m_norm_g[l], emit)
        y_a, y_ac = diff_attention(u[4], u[5], u[6], uc[4], uc[5], uc[6], a_qnorm_g[l], a_knorm_g[l], a_lambda[l],
                                   a_subln_g[l], lam_init, cos, sin, emit)
        y_s, y_sc = ssd_mixer(u[7], u[8], u[9], uc[7], uc[8], uc[9], s_conv_w[l], s_conv_b[l], s_dt_bias[l],
                              s_a_log[l], s_d[l], s_norm_g[l], emit)
        x = x + mod[2][:, None] * gated_merge(y_m, y_a, y_s, u[10], w_branch_m[l], w_branch_a[l], w_branch_s[l], w_out[l])

        h2 = modulate(rmsnorm(x, norm2_g[l]), mod[3][:, None], mod[4][:, None])
        x = x + mod[5][:, None] * conv_ffn(h2, w_up[l], ffn_conv_w[l], w_down[l])

        if emit:
            xc = xc + modc[2] * gated_merge(y_mc, y_ac, y_sc, uc[10], w_branch_m[l], w_branch_a[l], w_branch_s[l], w_out[l])
            hc2 = modulate(rmsnorm(xc, norm2_g[l]), modc[3], modc[4])
            xc = xc + modc[5] * conv_ffn(hc2, w_up[l], ffn_conv_w[l], w_down[l])
    return x
```

```python
import math
import numpy as np
import ml_dtypes
import concourse.bass as bass
import concourse.mybir as mybir
from concourse.bass_utils import run_bass_kernel_spmd

F32 = mybir.dt.float32
BF16 = mybir.dt.bfloat16
ALU = mybir.AluOpType
AF = mybir.ActivationFunctionType
AX = mybir.AxisListType

ENGS = ("sync", "scalar", "vector", "gpsimd", "tensor")
NDMA = 32

NCORE = 8
NB = 4
DEPTH = 2
D = 1024
TL = 2048
TC = 256
T = TL + TC
NT = T // 128
D_IN = 8224
D_FF = 2816
O_QK, O_V, O_O, O_MG, O_AQ, O_AK, O_AV, O_SZ, O_XBC, O_DT, O_BG = 0, 1024, 1536, 2048, 2064, 2576, 3088, 3600, 4112, 5136, 5152
WSPEC = [("w_mod", D, 6 * D), ("w_in", D, D_IN), ("w_branch_m", 512, D), ("w_branch_a", 512, D), ("w_branch_s", 512, D),
         ("w_out", D, D), ("w_up", D, 2 * D_FF), ("w_down", D_FF, D)]
WLAYER = sum(r * c for _, r, c in WSPEC)
WCOLS = 2048
WROWS = DEPTH * WLAYER // (NCORE * WCOLS)
assert WROWS * NCORE * WCOLS == DEPTH * WLAYER
USE_AG = False
TB = [(0, 256), (256, 512), (768, 512), (1280, 512), (1792, 512)]
SEGS = [(0, 256), (256, T)]


class Res:
    __slots__ = ("name", "t", "w", "r")

    def __init__(self, name, t):
        self.name = name
        self.t = t
        self.w = {}
        self.r = {}

    def __getitem__(self, k):
        return self.t[k]


class Prog:
    def __init__(self, nc):
        self.nc = nc
        self.ops = {e: [] for e in ENGS}
        self.cnt = {e: 0 for e in ENGS}
        self.seen = {e: {} for e in ENGS}
        self.dma_i = 0
        self.dma_use = [0] * NDMA
        self.stack = []
        self.esem = {}
        self.dsem = []
        self.n_ins = 0
        self.epoch = 0
        self.esems = {}
        self.ccs = []

    def enter(self, cm):
        v = cm.__enter__()
        self.stack.append(cm)
        return v

    def setup(self):
        self._new_esems()
        for i in range(NDMA):
            self.dsem.append(self.enter(self.nc.semaphore("ds_%d" % i)))

    def _new_esems(self):
        for e in ENGS:
            self.esem[e] = self.enter(self.nc.semaphore("es%d_%s" % (self.epoch, e)))
            self.esems[(e, self.epoch)] = self.esem[e]

    def new_epoch(self):
        self.barrier()
        self.epoch += 1
        self._new_esems()
        for e in ENGS:
            self.cnt[e] = 0

    def sbuf(self, name, shape, dt):
        return Res(name, self.enter(self.nc.sbuf_tensor(name, list(shape), dt)))

    def psum(self, name, shape, dt=F32):
        return Res(name, self.enter(self.nc.psum_tensor(name, list(shape), dt)))

    def dram(self, name, shape, dt, kind="Internal"):
        return Res(name, self.nc.dram_tensor(name, list(shape), dt, kind=kind).ap())

    def _deps(self, eng, reads, writes):
        need = {}
        for r in reads:
            for s, v in r.w.items():
                if need.get(s, 0) < v:
                    need[s] = v
        for w in writes:
            for s, v in w.w.items():
                if need.get(s, 0) < v:
                    need[s] = v
            for s, v in w.r.items():
                if need.get(s, 0) < v:
                    need[s] = v
        out = []
        seen = self.seen[eng]
        for s, v in need.items():
            if s[0] == "e" and s[1] == eng and eng in ("tensor", "sync"):
                continue
            if seen.get(s, 0) >= v:
                continue
            seen[s] = v
            out.append((s, v))
        return out

    def _semh(self, s):
        if s[0] == "c":
            return self.ccs[s[1]]
        return self.esems[(s[1], s[2])] if s[0] == "e" else self.dsem[s[1]]

    def _mark(self, tok, reads, writes):
        s, v = tok
        for r in reads:
            if r.r.get(s, 0) < v:
                r.r[s] = v
        for w in writes:
            w.w = {s: v}
            w.r = {}

    def op(self, eng, fn, reads=(), writes=()):
        waits = [(self._semh(s), v) for s, v in self._deps(eng, reads, writes)]
        self.cnt[eng] += 1
        n = self.cnt[eng]
        sem = self.esem[eng]
        self.n_ins += 1 + len(waits)

        def run(e, waits=waits, fn=fn, sem=sem):
            for sh, v in waits:
                e.wait_ge(sh, v)
            fn(e).then_inc(sem, 1)

        self.ops[eng].append(run)
        self._mark((("e", eng, self.epoch), n), reads, writes)

    def dma(self, eng, out_ap, in_ap, reads=(), writes=(), **kw):
        i = self.dma_i % NDMA
        self.dma_i += 1
        prev = self.dma_use[i] * 16
        self.dma_use[i] += 1
        val = self.dma_use[i] * 16
        deps = self._deps(eng, reads, writes)
        s = ("d", i)
        if prev and self.seen[eng].get(s, 0) < prev:
            self.seen[eng][s] = prev
            deps.append((s, prev))
        waits = [(self._semh(s_), v) for s_, v in deps]
        sem = self.dsem[i]
        self.n_ins += 1 + len(waits)

        def run(e, waits=waits, sem=sem, out_ap=out_ap, in_ap=in_ap, kw=kw):
            for sh, v in waits:
                e.wait_ge(sh, v)
            e.dma_start(out=out_ap, in_=in_ap, **kw).then_inc(sem, 16)

        self.ops[eng].append(run)
        self._mark((s, val), reads, writes)

    def allgather(self, src, dst, n):
        sem = self.enter(self.nc.semaphore("cc_%d" % len(self.ccs)))
        self.ccs.append(sem)
        key = ("c", len(self.ccs) - 1)
        deps = self._deps("gpsimd", [src], [dst])
        waits = [(self._semh(s_), v) for s_, v in deps]
        groups = [list(range(n))]

        def run(e, waits=waits, sem=sem):
            for sh, v in waits:
                e.wait_ge(sh, v)
            e.collective_compute("AllGather", ALU.bypass, replica_groups=groups, ins=[src[:]], outs=[dst[:]]).then_inc(sem, 16)

        self.ops["gpsimd"].append(run)
        self._mark((key, 16), [src], [dst])

    def barrier(self):
        finals = [(("d", i), self.dma_use[i] * 16) for i in range(NDMA) if self.dma_use[i]]
        finals += [(("c", i), 16) for i in range(len(self.ccs))]
        efin = [(("e", e, self.epoch), self.cnt[e]) for e in ENGS if self.cnt[e] and e != "sync"]
        for eng in ENGS:
            waits = []
            for s, v in finals + efin:
                if s[0] == "e" and s[1] == eng:
                    continue
                if self.seen[eng].get(s, 0) >= v:
                    continue
                self.seen[eng][s] = v
                waits.append((self._semh(s), v))
            self.n_ins += len(waits)

            def run(e, waits=waits):
                for sh, v in waits:
                    e.wait_ge(sh, v)

            self.ops[eng].append(run)

    def emit(self):
        with self.nc.Block() as block:
            for ename in ENGS:
                ops = self.ops[ename]

                def body(e, ops=ops):
                    for o in ops:
                        o(e)

                getattr(block, ename)(body)

    def close(self):
        while self.stack:
            self.stack.pop().__exit__(None, None, None)


class Pool:
    def __init__(self, tiles):
        self.tiles = tiles
        self.i = 0

    def next(self):
        t = self.tiles[self.i % len(self.tiles)]
        self.i += 1
        return t


class Builder:
    def __init__(self, nb=NB, layers=(0, 1), stop_after=None, dbg=(), skip=(), feed=(), only=None, ag=USE_AG):
        self.ag = ag
        self.feed = feed
        self.only = only
        self.nb = nb
        self.layers = layers
        self.skip = skip
        self.stop_after = stop_after
        self.dbg = dbg
        self.nc = bass.Bass("TRN2", target_bir_lowering=False, disable_frame_to_traceback=True)
        self.P = Prog(self.nc)
        self.P.setup()
        self.I = {}
        self.S = {}

    def inp(self, name, shape, dt=F32):
        if name in self.skip:
            shape = [1, 1]
        self.I[name] = self.P.dram(name, shape, dt, kind="ExternalInput")

    def scr(self, name, shape, dt=F32):
        kind = "ExternalOutput" if name in self.dbg else ("ExternalInput" if name in self.feed else "Internal")
        self.S[name] = self.P.dram(name, shape, dt, kind=kind)

    def declare(self):
        nb = self.nb
        inp = self.inp
        inp("x", [nb, TL, D]); inp("ctx", [nb, TC, D]); inp("cT", [128, 8, 5])
        if not self.ag:
            for nm, r, c in WSPEC:
                inp(nm, [DEPTH, r, c])
        else:
            inp("wsh", [WROWS, WCOLS])
            self.scr("wbounce", [WROWS, WCOLS]); self.scr("wall", [NCORE * WROWS, WCOLS])
            wt = self.S["wall"].t.tensor
            off = 0
            for nm, r, c in WSPEC:
                self.I[nm] = Res(nm, bass.AP(wt, off, [[WLAYER, DEPTH], [c, r], [1, c]]))
                off += r * c
        inp("b_mod", [DEPTH, 6 * D])
        inp("norm1_g", [DEPTH, D]); inp("norm2_g", [DEPTH, D])
        inp("m_cw", [DEPTH, 128, 8, 4]); inp("m_gb", [DEPTH, 16]); inp("m_norm_g", [DEPTH, 512])
        inp("a_qnorm_g", [DEPTH, 64]); inp("a_knorm_g", [DEPTH, 64]); inp("a_lambda", [DEPTH, 256]); inp("a_subln_g", [DEPTH, 128])
        inp("s_cw", [DEPTH, 128, 8, 4]); inp("s_dt_bias", [DEPTH, 16]); inp("s_a_log", [DEPTH, 16]); inp("s_d", [DEPTH, 512]); inp("s_norm_g", [DEPTH, 512])
        inp("f_cw", [DEPTH, 128, 44, 3])
        inp("k_ident", [128, 128]); inp("k_tri", [2, 128, 128]); inp("k_neg", [2, 128, 512]); inp("k_cos", [TL, 32]); inp("k_sin", [TL, 32])
        self.out = self.P.dram("y", [nb, TL, D], F32, kind="ExternalOutput")
        scr = self.scr
        scr("xres", [T, D]); scr("modv", [DEPTH, 5, 6 * D])
        scr("mqkT", [1024, T], BF16); scr("mv", [T, 512], BF16); scr("mo", [T, 512]); scr("mg", [T, 16])
        scr("aq", [T, 512]); scr("ak", [T, 512]); scr("av", [T, 512], BF16)
        scr("sz", [T, 512]); scr("sxT", [1024, T], BF16); scr("sdt", [T, 16]); scr("bgT", [3072, T], BF16)
        scr("yT", [3, 512, T], BF16); scr("actT", [D_FF, T], BF16)
        P = self.P
        self.ident_f = P.sbuf("ident_f", [128, 128], F32)
        self.ident_b = P.sbuf("ident_b", [128, 128], BF16)
        self.ones_f = P.sbuf("ones_f", [128, 128], F32)
        self.tri = P.sbuf("tri", [128, 2, 128], F32)
        self.neg = P.sbuf("neg", [128, 2, 512], F32)
        self.cTs = P.sbuf("cTs", [128, 8, 5], F32)
        self.pm = Pool([P.psum("pm%d" % i, [128, 512], F32) for i in range(2)])
        self.pt = Pool([P.psum("pt%d" % i, [128, 8, 128], BF16) for i in range(2)])
        self.px = [P.psum("px%d" % i, [128, 512], F32) for i in range(4)]
        self.arena = self.P.enter(self.nc.sbuf_tensor("arena", [128, 49152], F32))
        self.aoff = 0

    def reset_arena(self):
        self.P.barrier()
        self.aoff = 0

    def tile(self, name, shape, dt):
        n = 1
        for s in shape[1:]:
            n *= s
        nbytes = n * (4 if dt == F32 else 2)
        nw = (nbytes + 3) // 4
        assert self.aoff + nw <= 49152, (name, self.aoff, nw)
        v = self.arena[0:shape[0], self.aoff:self.aoff + nw]
        self.aoff += nw
        if dt != F32:
            v = v.bitcast(dt)
            v = v[:, 0:n]
        if len(shape) == 3:
            v = v.rearrange("p (a b) -> p a b", a=shape[1])
        elif len(shape) == 4:
            v = v.rearrange("p (a b c) -> p a b c", a=shape[1], b=shape[2])
        return Res(name, v)

    def tpool(self, name, shape, dt, n):
        return Pool([self.tile("%s%d" % (name, i), shape, dt) for i in range(n)])

    def load_w(self, wap, kc, ncols, src):
        P = self.P
        st = self.wst.next()
        wb = self.wbf.next()
        P.dma("sync", st[:, 0:kc, 0:ncols], wap.rearrange("(c p) n -> p c n", p=128), reads=[src], writes=[st])
        P.op("gpsimd", lambda e: e.tensor_copy(wb[:, 0:kc, 0:ncols], st[:, 0:kc, 0:ncols]), reads=[st], writes=[wb])
        return wb

    def consts(self):
        P, I = self.P, self.I
        if self.ag:
            S = self.S
            P.dma("sync", S["wbounce"][:], I["wsh"][:], reads=[I["wsh"]], writes=[S["wbounce"]])
            P.allgather(S["wbounce"], S["wall"], NCORE)
            for nm, _, _ in WSPEC:
                I[nm].w = dict(S["wall"].w)
        P.dma("sync", self.ident_f[:], I["k_ident"][:], reads=[I["k_ident"]], writes=[self.ident_f])
        P.dma("sync", self.tri[:], I["k_tri"].t.rearrange("d p t -> p d t"), reads=[I["k_tri"]], writes=[self.tri])
        P.dma("sync", self.neg[:], I["k_neg"].t.rearrange("d p t -> p d t"), reads=[I["k_neg"]], writes=[self.neg])
        P.dma("sync", self.cTs[:], I["cT"][:], reads=[I["cT"]], writes=[self.cTs])
        P.op("vector", lambda e: e.tensor_copy(self.ident_b[:], self.ident_f[:]), reads=[self.ident_f], writes=[self.ident_b])
        P.op("vector", lambda e: e.memset(self.ones_f[:], 1.0), writes=[self.ones_f])
        P.op("scalar", lambda e: e.activation(self.cTs[:], self.cTs[:], AF.Silu), reads=[self.cTs], writes=[self.cTs])

    def stage_mod(self):
        P, I, S = self.P, self.I, self.S
        self.reset_arena()
        wst = self.tpool("mw", [128, 8, 512], F32, 2)
        bm = self.tpool("mb", [5, 512], F32, 2)
        mo = self.tpool("mo", [5, 512], F32, 2)
        for l in self.layers:
            for cb in range(12):
                w = wst.next()
                P.dma("sync", w[:], I["w_mod"].t[l, :, cb * 512:(cb + 1) * 512].rearrange("(c p) n -> p c n", p=128), reads=[I["w_mod"]], writes=[w])
                b = bm.next()
                P.dma("sync", b[:], I["b_mod"].t[l, cb * 512:(cb + 1) * 512].partition_broadcast(5), reads=[I["b_mod"]], writes=[b])
                ps = self.pm.next()
                for kc in range(8):
                    P.op("tensor", lambda e, ps=ps, w=w, kc=kc: e.matmul(ps[0:5, :], self.cTs[:, kc, :], w[:, kc, :], start=(kc == 0), stop=(kc == 7)),
                         reads=[self.cTs, w], writes=[ps])
                o = mo.next()
                P.op("vector", lambda e, o=o, ps=ps, b=b: e.tensor_tensor(o[:], ps[0:5, :], b[:], ALU.add), reads=[ps, b], writes=[o])
                P.dma("scalar", S["modv"].t[l, :, cb * 512:(cb + 1) * 512], o[:], reads=[o], writes=[S["modv"]])

    def norm_stage(self, l, bi, gname, m_shift, m_scale, hT, tiles):
        P, I, S = self.P, self.I, self.S
        A = [self.tile("nA%d" % v, [128, D], F32) for v in range(2)]
        Sh = [self.tile("nS%d" % v, [128, D], F32) for v in range(2)]
        gt = self.tile("ng", [128, D], F32)
        P.dma("sync", gt[:], I[gname].t[l, :].partition_broadcast(128), reads=[I[gname]], writes=[gt])
        for v, row in ((0, 4), (1, bi)):
            P.dma("sync", A[v][:], S["modv"].t[l, row, m_scale * D:(m_scale + 1) * D].partition_broadcast(128), reads=[S["modv"]], writes=[A[v]])
            P.dma("sync", Sh[v][:], S["modv"].t[l, row, m_shift * D:(m_shift + 1) * D].partition_broadcast(128), reads=[S["modv"]], writes=[Sh[v]])
            P.op("vector", lambda e, v=v: e.scalar_tensor_tensor(A[v][:], A[v][:], 1.0, gt[:], ALU.add, ALU.mult), reads=[A[v], gt], writes=[A[v]])
        xt = self.tpool("nx", [128, D], F32, 2)
        sq = self.tpool("nsq", [128, D], F32, 2)
        hb = self.tpool("nhb", [128, D], BF16, 2)
        st = self.tpool("nst", [128, 2], F32, 2)
        for i in tiles:
            v = 0 if i < 2 else 1
            x = xt.next(); q = sq.next(); h = hb.next(); s = st.next()
            P.dma("sync", x[:], S["xres"].t[i * 128:(i + 1) * 128, :], reads=[S["xres"]], writes=[x])
            P.op("scalar", lambda e, q=q, x=x: e.activation(q[:], x[:], AF.Square), reads=[x], writes=[q])
            P.op("vector", lambda e, q=q, s=s: e.tensor_reduce(s[:, 0:1], q[:], axis=AX.X, op=ALU.add), reads=[q], writes=[s])
            P.op("scalar", lambda e, s=s: e.activation(s[:, 1:2], s[:, 0:1], AF.Sqrt, bias=1e-6, scale=1.0 / D), reads=[s], writes=[s])
            P.op("vector", lambda e, s=s: e.reciprocal(s[:, 1:2], s[:, 1:2]), reads=[s], writes=[s])
            P.op("vector", lambda e, q=q, x=x, s=s, v=v: e.scalar_tensor_tensor(q[:], x[:], s[:, 1:2], A[v][:], ALU.mult, ALU.mult), reads=[x, s, A[v]], writes=[q])
            P.op("vector", lambda e, q=q, h=h, v=v: e.tensor_tensor(h[:], q[:], Sh[v][:], ALU.add), reads=[q, Sh[v]], writes=[h])
            ps = self.pt.next()
            for c in range(8):
                P.op("tensor", lambda e, ps=ps, h=h, c=c: e.transpose(ps[:, c, :], h[:, c * 128:(c + 1) * 128], self.ident_b[:]), reads=[h, self.ident_b], writes=[ps])
            P.op("scalar", lambda e, ps=ps, i=i: e.copy(hT[:, :, i * 128:(i + 1) * 128], ps[:]), reads=[ps], writes=[hT])

    def proj_tm(self, hT, kc, wsrc, wap_fn, ncols, tiles, epi):
        P = self.P
        for c0 in range(0, ncols, 512):
            cw = min(512, ncols - c0)
            wb = self.load_w(wap_fn(c0, cw), kc, cw, wsrc)
            for i in tiles:
                ps = self.pm.next()
                for k in range(kc):
                    P.op("tensor", lambda e, ps=ps, wb=wb, k=k, i=i, cw=cw: e.matmul(ps[:, 0:cw], hT[:, k, i * 128:(i + 1) * 128], wb[:, k, 0:cw], start=(k == 0), stop=(k == kc - 1)),
                         reads=[hT, wb], writes=[ps])
                epi(i, c0, cw, ps)

    def proj_fm(self, hT, kc, wsrc, wap_fn, nfeat, tbs, epi, epi_end=None):
        P = self.P
        for c0 in range(0, nfeat, 512):
            cw = min(512, nfeat - c0)
            wb = self.load_w(wap_fn(c0, cw), kc, cw, wsrc)
            for j in range(cw // 128):
                fb = c0 // 128 + j
                for (t0, tw) in tbs:
                    ps = self.pm.next()
                    for k in range(kc):
                        P.op("tensor", lambda e, ps=ps, wb=wb, k=k, j=j, t0=t0, tw=tw: e.matmul(ps[:, 0:tw], wb[:, k, j * 128:(j + 1) * 128], hT[:, k, t0:t0 + tw], start=(k == 0), stop=(k == kc - 1)),
                             reads=[hT, wb], writes=[ps])
                    epi(fb, t0, tw, ps)
                if epi_end is not None:
                    epi_end(fb)

    def conv_rows(self, rb, acc, cw, blk, has_bias, segs):
        P = self.P
        for (s0, s1) in segs:
            if has_bias:
                P.op("vector", lambda e, s0=s0, s1=s1: e.tensor_scalar(acc[:, s0:s1], rb[:, s0:s1], cw[:, blk, 1:2], cw[:, blk, 3:4], ALU.mult, ALU.add), reads=[rb, cw], writes=[acc])
            else:
                P.op("vector", lambda e, s0=s0, s1=s1: e.tensor_scalar(acc[:, s0:s1], rb[:, s0:s1], cw[:, blk, 1:2], None, ALU.mult), reads=[rb, cw], writes=[acc])
            P.op("vector", lambda e, s0=s0, s1=s1: e.scalar_tensor_tensor(acc[:, s0 + 1:s1], rb[:, s0:s1 - 1], cw[:, blk, 0:1], acc[:, s0 + 1:s1], ALU.mult, ALU.add), reads=[rb, cw, acc], writes=[acc])
            P.op("vector", lambda e, s0=s0, s1=s1: e.scalar_tensor_tensor(acc[:, s0:s1 - 1], rb[:, s0 + 1:s1], cw[:, blk, 2:3], acc[:, s0:s1 - 1], ALU.mult, ALU.add), reads=[rb, cw, acc], writes=[acc])

    def stage_inproj(self, l, bi):
        P, I, S = self.P, self.I, self.S
        self.reset_arena()
        hT = self.tile("hT", [128, 8, T], BF16)
        self.wst = self.tpool("wst", [128, 8, 512], F32, 2)
        self.wbf = self.tpool("wbf", [128, 8, 512], BF16, 2)
        mark = self.aoff
        self.norm_stage(l, bi, "norm1_g", 0, 1, hT, range(NT))
        self.P.barrier()
        self.aoff = mark
        W = I["w_in"]
        tiles = range(NT)
        stf = self.tpool("stf", [128, 512], F32, 3)
        stb = self.tpool("stb", [128, 512], BF16, 3)

        def wfn(base):
            return lambda c0, cw: W.t[l, :, base + c0:base + c0 + cw]

        def epi_store(dst, bf, act=None, eng="vector"):
            def epi(i, c0, cw, ps):
                st = (stb if bf else stf).next()
                if act is None:
                    if eng == "vector":
                        P.op("vector", lambda e: e.tensor_copy(st[:, 0:cw], ps[:, 0:cw]), reads=[ps], writes=[st])
                    else:
                        P.op("scalar", lambda e: e.copy(st[:, 0:cw], ps[:, 0:cw]), reads=[ps], writes=[st])
                else:
                    P.op("scalar", lambda e: e.activation(st[:, 0:cw], ps[:, 0:cw], act), reads=[ps], writes=[st])
                P.dma("scalar", S[dst].t[i * 128:(i + 1) * 128, c0:c0 + cw], st[:, 0:cw], reads=[st], writes=[S[dst]])
            return epi

        self.proj_tm(hT, 8, W, wfn(O_V), 512, tiles, epi_store("mv", True))
        self.proj_tm(hT, 8, W, wfn(O_O), 512, tiles, epi_store("mo", False, AF.Sigmoid))
        self.proj_tm(hT, 8, W, wfn(O_MG), 16, tiles, epi_store("mg", False))
        self.proj_tm(hT, 8, W, wfn(O_AQ), 512, tiles, epi_store("aq", False, eng="scalar"))
        self.proj_tm(hT, 8, W, wfn(O_AK), 512, tiles, epi_store("ak", False))
        self.proj_tm(hT, 8, W, wfn(O_AV), 512, tiles, epi_store("av", True, eng="scalar"))
        self.proj_tm(hT, 8, W, wfn(O_SZ), 512, tiles, epi_store("sz", False, AF.Silu))
        self.proj_tm(hT, 8, W, wfn(O_DT), 16, tiles, epi_store("sdt", False))

        rbp = self.tpool("rb", [128, T], F32, 2)
        accp = self.tpool("acc", [128, T], F32, 2)
        obp = self.tpool("ob", [128, T], BF16, 2)
        cwm = self.tile("cwm", [128, 8, 4], F32)
        cws = self.tile("cws", [128, 8, 4], F32)
        P.dma("sync", cwm[:], I["m_cw"].t[l], reads=[I["m_cw"]], writes=[cwm])
        P.dma("sync", cws[:], I["s_cw"].t[l], reads=[I["s_cw"]], writes=[cws])
        for (base, cw_t, dst, kscale) in ((O_QK, cwm, "mqkT", True), (O_XBC, cws, "sxT", False)):
            cur = {}

            def epi(fb, t0, tw, ps, cur=cur):
                if t0 == 0:
                    cur["rb"] = rbp.next()
                rb = cur["rb"]
                P.op("scalar", lambda e: e.copy(rb[:, t0:t0 + tw], ps[:, 0:tw]), reads=[ps], writes=[rb])

            def epi_end(fb, cur=cur, cw_t=cw_t, dst=dst, kscale=kscale):
                rb = cur["rb"]
                acc = accp.next()
                ob = obp.next()
                self.conv_rows(rb, acc, cw_t, fb, True, SEGS)
                if kscale and fb >= 4:
                    P.op("scalar", lambda e: e.activation(acc[:], acc[:], AF.Silu), reads=[acc], writes=[acc])
                    P.op("gpsimd", lambda e: e.tensor_scalar(ob[:], acc[:], 128.0 ** -0.5, None, ALU.mult), reads=[acc], writes=[ob])
                else:
                    P.op("scalar", lambda e: e.activation(ob[:], acc[:], AF.Silu), reads=[acc], writes=[ob])
                P.dma("scalar", S[dst].t[fb * 128:(fb + 1) * 128, :], ob[:], reads=[ob], writes=[S[dst]])

            self.proj_fm(hT, 8, W, wfn(base), 1024, TB, epi, epi_end)

        def epi_g(fb, t0, tw, ps):
            st = stb.next()
            P.op("scalar", lambda e: e.activation(st[:, 0:tw], ps[:, 0:tw], AF.Sigmoid), reads=[ps], writes=[st])
            P.dma("scalar", S["bgT"].t[fb * 128:(fb + 1) * 128, t0:t0 + tw], st[:, 0:tw], reads=[st], writes=[S["bgT"]])

        self.proj_fm(hT, 8, W, wfn(O_BG), 3072, TB, epi_g)

    def fin_setup(self):
        self.f_sq = self.tpool("fsq", [128, 512], F32, 2)
        self.f_t = self.tpool("ft", [128, 512], F32, 2)
        self.f_st = self.tpool("fst", [128, 8], F32, 2)
        self.f_yb = self.tpool("fyb", [128, 512], BF16, 2)
        self.f_tb = self.tpool("ftb", [128, 4, 128], BF16, 2)

    def finalize(self, src, src_res, ng, gain, extra, br, i):
        P, S = self.P, self.S
        gs = 512 // ng
        q = self.f_sq.next(); t = self.f_t.next(); st = self.f_st.next(); yb = self.f_yb.next(); tb = self.f_tb.next()
        P.op("scalar", lambda e: e.activation(q[:], src, AF.Square), reads=[src_res], writes=[q])
        P.op("vector", lambda e: e.tensor_reduce(st[:, 0:ng], q[:].rearrange("p (g d) -> p g d", g=ng), axis=AX.X, op=ALU.add), reads=[q], writes=[st])
        P.op("scalar", lambda e: e.activation(st[:, 4:4 + ng], st[:, 0:ng], AF.Sqrt, bias=1e-6, scale=1.0 / gs), reads=[st], writes=[st])
        P.op("vector", lambda e: e.reciprocal(st[:, 4:4 + ng], st[:, 4:4 + ng]), reads=[st], writes=[st])
        P.op("vector", lambda e: e.tensor_tensor(t[:].rearrange("p (g d) -> p g d", g=ng), src.rearrange("p (g d) -> p g d", g=ng),
                                                 st[:, 4:4 + ng].unsqueeze(2).to_broadcast([128, ng, gs]), ALU.mult), reads=[src_res, st], writes=[t])
        if extra is None:
            P.op("gpsimd", lambda e: e.tensor_tensor(yb[:], t[:], gain[:], ALU.mult), reads=[t, gain], writes=[yb])
        else:
            P.op("gpsimd", lambda e: e.tensor_tensor(t[:], t[:], gain[:], ALU.mult), reads=[t, gain], writes=[t])
            P.op("vector", lambda e: e.tensor_tensor(yb[:], t[:], extra[:], ALU.mult), reads=[t, extra], writes=[yb])
        ps = self.pt.next()
        for c in range(4):
            P.op("tensor", lambda e, c=c: e.transpose(ps[:, c, :], yb[:, c * 128:(c + 1) * 128], self.ident_b[:]), reads=[yb, self.ident_b], writes=[ps])
        P.op("scalar", lambda e: e.copy(tb[:], ps[:, 0:4, :]), reads=[ps], writes=[tb])
        P.dma("scalar", S["yT"].t[br, :, i * 128:(i + 1) * 128].rearrange("(c p) t -> p c t", p=128), tb[:], reads=[tb], writes=[S["yT"]])

    def stage_scan(self, l, bi, kind):
        P, I, S = self.P, self.I, self.S
        self.reset_arena()
        m = kind == "m"
        NH = 4 if m else 8
        NG = 4 if m else 2
        nh = 2 if m else 4
        dv = 129 if m else 64
        dvp = 130 if m else 64
        src = S["mqkT"] if m else S["sxT"]
        qrow, krow = (0, 512) if m else (768, 512)
        QT = self.tile("QT", [128, NG, T], BF16)
        KT = self.tile("KT", [128, NG, T], BF16)
        Ktm = self.tile("Ktm", [128, NT, NG, 128], BF16)
        V = self.tile("V", [128, NT, NH, dvp], BF16)
        yacc = self.tile("yacc", [128, NT, 512], F32)
        G = self.tile("G", [128, NT, 16], F32)
        Z = self.tile("Z", [128, NT, 16], F32)
        L1 = self.tile("L1", [128, NT, 16], F32)
        L2 = self.tile("L2", [128, NT, 16], F32)
        gb = self.tile("gb", [128, 16], F32)
        Hs = [self.tile("H%d" % d_, [128, NH, dvp], F32) for d_ in range(2)]
        Hbfs = [self.tile("Hbf%d" % d_, [128, NH, dvp], BF16) for d_ in range(2)]
        P.dma("sync", QT[:], src.t[qrow:qrow + NG * 128, :].rearrange("(g p) t -> p g t", p=128), reads=[src], writes=[QT])
        P.dma("sync", KT[:], src.t[krow:krow + NG * 128, :].rearrange("(g p) t -> p g t", p=128), reads=[src], writes=[KT])
        gsrc = S["mg"] if m else S["sdt"]
        P.dma("sync", G[:], gsrc.t.rearrange("(i p) c -> p i c", p=128), reads=[gsrc], writes=[G])
        bsrc = I["m_gb"] if m else I["s_dt_bias"]
        P.dma("sync", gb[:], bsrc.t[l, :].partition_broadcast(128), reads=[bsrc], writes=[gb])
        gbb = gb[:, :].unsqueeze(1).to_broadcast([128, NT, 16])
        P.op("vector", lambda e: e.tensor_tensor(Z[:], G[:], gbb, ALU.add), reads=[G, gb], writes=[Z])
        if m:
            P.op("scalar", lambda e: e.activation(L1[:], Z[:], AF.Exp, scale=-1.0), reads=[Z], writes=[L1])
            P.op("scalar", lambda e: e.activation(L1[:], L1[:], AF.Ln, bias=1.0), reads=[L1], writes=[L1])
            P.op("vector", lambda e: e.tensor_scalar(L1[:], L1[:], -1.0, None, ALU.mult), reads=[L1], writes=[L1])
            LFv = lambda i, d: L1[:, i, d * 8 + 4:d * 8 + 8]
            IGv = lambda i, d: Z[:, i, d * 8:d * 8 + 4]
            LFr, IGr = L1, Z
            for i in range(NT):
                P.dma("sync", V[:, i, :, 0:128], S["mv"].t[i * 128:(i + 1) * 128, :].rearrange("p (h d) -> p h d", h=4), reads=[S["mv"]], writes=[V])
            P.op("vector", lambda e: e.memset(V[:, :, :, 128:130], 1.0), writes=[V])
        else:
            al = self.tile("al", [128, 16], F32)
            P.dma("sync", al[:], I["s_a_log"].t[l, :].partition_broadcast(128), reads=[I["s_a_log"]], writes=[al])
            P.op("scalar", lambda e: e.activation(al[:], al[:], AF.Exp), reads=[al], writes=[al])
            P.op("vector", lambda e: e.tensor_scalar(al[:], al[:], -1.0, None, ALU.mult), reads=[al], writes=[al])
            P.op("scalar", lambda e: e.activation(L2[:], Z[:], AF.Exp), reads=[Z], writes=[L2])
            P.op("scalar", lambda e: e.activation(L2[:], L2[:], AF.Ln, bias=1.0), reads=[L2], writes=[L2])
            P.op("vector", lambda e: e.tensor_tensor(L1[:], L2[:], al[:, :].unsqueeze(1).to_broadcast([128, NT, 16]), ALU.mult), reads=[L2, al], writes=[L1])
            P.op("scalar", lambda e: e.activation(Z[:], L2[:], AF.Ln), reads=[L2], writes=[Z])
            LFv = lambda i, d: L1[:, i, d * 8:d * 8 + 8]
            IGv = lambda i, d: Z[:, i, d * 8:d * 8 + 8]
            LFr, IGr = L1, Z
            XT = self.tile("XT", [128, 4, T], BF16)
            P.dma("sync", XT[:], src.t[0:512, :].rearrange("(g p) t -> p g t", p=128), reads=[src], writes=[XT])
            for i in range(NT):
                ps = self.pt.next()
                for c in range(4):
                    P.op("tensor", lambda e, ps=ps, c=c, i=i: e.transpose(ps[:, c, :], XT[:, c, i * 128:(i + 1) * 128], self.ident_b[:]), reads=[XT, self.ident_b], writes=[ps])
                P.op("scalar", lambda e, ps=ps, i=i: e.copy(V[:, i, :, :].rearrange("p h d -> p (h d)"), ps[:, 0:4, :].rearrange("p c d -> p (c d)")), reads=[ps], writes=[V])
        for i in range(NT):
            ps = self.pt.next()
            for g in range(NG):
                P.op("tensor", lambda e, ps=ps, g=g, i=i: e.transpose(ps[:, g, :], KT[:, g, i * 128:(i + 1) * 128], self.ident_b[:]), reads=[KT, self.ident_b], writes=[ps])
            P.op("vector", lambda e, ps=ps, i=i: e.tensor_copy(Ktm[:, i, :, :], ps[:, 0:NG, :]), reads=[ps], writes=[Ktm])

        Einp = self.tpool("Ein", [128, 3, NH], F32, 2)
        Ep = self.tpool("E", [128, 3, NH], F32, 2)
        Avp = self.tpool("Av", [128, NH], F32, 2)
        dgp = self.tpool("dg", [128, nh, 128], F32, 2)
        Mp = self.tpool("M", [128, nh, 128], F32, 2)
        PTp = self.tpool("PT", [128, nh, 128], BF16, 2)
        Vwp = self.tpool("Vw", [128, nh, dvp], BF16, 2)
        Yp = self.tpool("Y", [128, nh, dvp], F32, 2)
        dnp = self.tpool("dn", [128, nh], F32, 2)
        px = self.px
        def chunk(d, i, first):
            if True:
                c0 = i * 128
                cps = self.pm.next()
                lf, ig = LFv(i, d), IGv(i, d)
                P.op("tensor", lambda e, cps=cps, lf=lf, d=d: e.matmul(cps[:, 0:NH], self.tri[:, d, :], lf, start=True, stop=True), reads=[self.tri, LFr], writes=[cps])
                P.op("tensor", lambda e, cps=cps, lf=lf: e.matmul(cps[:, NH:2 * NH], self.ones_f[:], lf, start=True, stop=True), reads=[self.ones_f, LFr], writes=[cps])
                Ein = Einp.next(); E = Ep.next(); Av = Avp.next()
                P.op("vector", lambda e, Ein=Ein, cps=cps: e.tensor_copy(Ein[:, 0, :], cps[:, 0:NH]), reads=[cps], writes=[Ein])
                P.op("vector", lambda e, Av=Av, cps=cps, ig=ig: e.tensor_tensor(Av[:], ig, cps[:, 0:NH], ALU.subtract), reads=[cps, IGr], writes=[Av])
                P.op("vector", lambda e, Ein=Ein, Av=Av, cps=cps: e.tensor_tensor(Ein[:, 1, :], Av[:], cps[:, NH:2 * NH], ALU.add), reads=[cps, Av], writes=[Ein])
                P.op("vector", lambda e, Ein=Ein, cps=cps: e.tensor_copy(Ein[:, 2, :], cps[:, NH:2 * NH]), reads=[cps], writes=[Ein])
                P.op("scalar", lambda e, E=E, Ein=Ein: e.activation(E[:], Ein[:], AF.Exp), reads=[Ein], writes=[E])
                for gi in range(NH // nh):
                    group(d, i, first, c0, gi, Ein, E, Av)

        def group(d, i, first, c0, gi, Ein, E, Av):
            H, Hbf = Hs[d], Hbfs[d]
            if True:
                if True:
                    h0 = gi * nh
                    sps, dps, yi, yo = px[0], px[1], px[2], px[3]
                    hps = self.pm.next()
                    if m:
                        for j in range(nh):
                            P.op("tensor", lambda e, j=j, h=h0 + j: e.matmul(sps[:, j * 128:(j + 1) * 128], KT[:, h, c0:c0 + 128], QT[:, h, c0:c0 + 128], start=True, stop=True), reads=[KT, QT], writes=[sps])
                        spv = sps[:, 0:nh * 128].rearrange("p (j t) -> p j t", j=nh)
                    else:
                        P.op("tensor", lambda e: e.matmul(sps[:, 0:128], KT[:, gi, c0:c0 + 128], QT[:, gi, c0:c0 + 128], start=True, stop=True), reads=[KT, QT], writes=[sps])
                        spv = sps[:, 0:128].unsqueeze(1).to_broadcast([128, nh, 128])
                    dg = dgp.next(); M = Mp.next(); PT = PTp.next(); Vw = Vwp.next(); Y = Yp.next()
                    P.op("vector", lambda e, dg=dg, Ein=Ein: e.tensor_tensor(dg[:], self.ident_f[:, :].unsqueeze(1).to_broadcast([128, nh, 128]),
                                                                     Ein[:, 0, h0:h0 + nh].unsqueeze(2).to_broadcast([128, nh, 128]), ALU.mult), reads=[self.ident_f, Ein], writes=[dg])
                    P.op("tensor", lambda e, dg=dg: e.matmul(dps[:, 0:nh * 128], self.ones_f[:], dg[:].rearrange("p j t -> p (j t)"), start=True, stop=False), reads=[self.ones_f, dg], writes=[dps])
                    P.op("tensor", lambda e: e.matmul(dps[:, 0:nh * 128], self.ident_f[:], self.neg[:, d, 0:nh * 128], start=False, stop=True), reads=[self.ident_f, self.neg], writes=[dps])
                    for j in range(nh):
                        P.op("scalar", lambda e, j=j, M=M, Av=Av: e.activation(M[:, j, :], dps[:, j * 128:(j + 1) * 128], AF.Exp, bias=Av[:, h0 + j:h0 + j + 1]), reads=[dps, Av], writes=[M])
                    P.op("vector", lambda e, PT=PT, M=M, spv=spv: e.tensor_tensor(PT[:], spv, M[:], ALU.mult), reads=[sps, M], writes=[PT])
                    P.op("gpsimd", lambda e, Vw=Vw, E=E: e.tensor_tensor(Vw[:, :, 0:dv], V[:, i, h0:h0 + nh, 0:dv], E[:, 1, h0:h0 + nh].unsqueeze(2).to_broadcast([128, nh, dv]), ALU.mult), reads=[V, E], writes=[Vw])
                    for j in range(nh):
                        h = h0 + j
                        qg = h if m else gi
                        P.op("tensor", lambda e, j=j, h=h, PT=PT: e.matmul(yi[:, j * dvp:j * dvp + dv], PT[:, j, :], V[:, i, h, 0:dv], start=True, stop=True), reads=[PT, V], writes=[yi])
                        if not first:
                            P.op("tensor", lambda e, j=j, h=h, qg=qg: e.matmul(yo[:, j * dvp:j * dvp + dv], QT[:, qg, c0:c0 + 128], Hbf[:, h, 0:dv], start=True, stop=True), reads=[QT, Hbf], writes=[yo])
                        P.op("tensor", lambda e, j=j, qg=qg, Vw=Vw, hps=hps: e.matmul(hps[:, j * dvp:j * dvp + dv], Ktm[:, i, qg, :], Vw[:, j, 0:dv], start=True, stop=True), reads=[Ktm, Vw], writes=[hps])
                    yiv = yi[:, 0:nh * dvp].rearrange("p (j v) -> p j v", j=nh)
                    P.op("scalar", lambda e, Y=Y, yiv=yiv: e.copy(Y[:, :, 0:dv], yiv[:, :, 0:dv]), reads=[yi], writes=[Y])
                    if not first:
                        for j in range(nh):
                            P.op("vector", lambda e, j=j, Y=Y, E=E: e.scalar_tensor_tensor(Y[:, j, 0:dv], yo[:, j * dvp:j * dvp + dv], E[:, 0, h0 + j:h0 + j + 1], Y[:, j, 0:dv], ALU.mult, ALU.add), reads=[yo, E, Y], writes=[Y])
                    if m:
                        dn = dnp.next()
                        P.op("vector", lambda e, dn=dn, Y=Y: e.tensor_scalar(dn[:], Y[:, :, 128], 1.0, None, ALU.max), reads=[Y], writes=[dn])
                        P.op("vector", lambda e, dn=dn, Y=Y: e.scalar_tensor_tensor(dn[:], Y[:, :, 128], -1.0, dn[:], ALU.mult, ALU.max), reads=[Y, dn], writes=[dn])
                        P.op("vector", lambda e, dn=dn: e.reciprocal(dn[:], dn[:]), reads=[dn], writes=[dn])
                        for j in range(nh):
                            ya = yacc[:, i, (h0 + j) * 128:(h0 + j + 1) * 128]
                            P.op("vector", lambda e, j=j, ya=ya, Y=Y, dn=dn: e.scalar_tensor_tensor(ya, Y[:, j, 0:128], dn[:, j:j + 1], ya, ALU.mult, ALU.add), reads=[Y, dn, yacc], writes=[yacc])
                    else:
                        ya = yacc[:, i, h0 * 64:(h0 + nh) * 64].rearrange("p (j v) -> p j v", j=nh)
                        P.op("gpsimd", lambda e, ya=ya, Y=Y: e.tensor_tensor(ya, ya, Y[:], ALU.add), reads=[Y, yacc], writes=[yacc])
                    hpv = hps[:, 0:nh * dvp].rearrange("p (j v) -> p j v", j=nh)
                    if first:
                        P.op("scalar", lambda e, hpv=hpv: e.copy(H[:, h0:h0 + nh, 0:dv], hpv[:, :, 0:dv]), reads=[hps], writes=[H])
                    else:
                        for j in range(nh):
                            P.op("vector", lambda e, j=j, E=E, hps=hps: e.scalar_tensor_tensor(H[:, h0 + j, 0:dv], H[:, h0 + j, 0:dv], E[:, 2, h0 + j:h0 + j + 1], hps[:, j * dvp:j * dvp + dv], ALU.mult, ALU.add), reads=[H, E, hps], writes=[H])
                    P.op("gpsimd", lambda e: e.tensor_copy(Hbf[:, h0:h0 + nh, 0:dv], H[:, h0:h0 + nh, 0:dv]), reads=[H], writes=[Hbf])

        P.op("gpsimd", lambda e: e.memset(yacc[:], 0.0), writes=[yacc])
        orders = [list(range(NT)), [1, 0] + list(range(NT - 1, 1, -1))]
        for n_ in range(NT):
            for d in range(2):
                chunk(d, orders[d][n_], n_ == 0)

        self.fin_setup()
        gain = self.tile("gain", [128, 512], F32)
        gsrc2 = I["m_norm_g"] if m else I["s_norm_g"]
        P.dma("sync", gain[:], gsrc2.t[l, :].partition_broadcast(128), reads=[gsrc2], writes=[gain])
        exp_ = self.tpool("ex", [128, 512], F32, 2)
        tiles = range(NT) if l < DEPTH - 1 else range(2, NT)
        if m:
            for i in tiles:
                ex = exp_.next()
                P.dma("sync", ex[:], S["mo"].t[i * 128:(i + 1) * 128, :], reads=[S["mo"]], writes=[ex])
                self.finalize(yacc[:, i, :], yacc, 4, gain, ex, 0, i)
        else:
            dsk = self.tile("dsk", [128, 512], F32)
            P.dma("sync", dsk[:], I["s_d"].t[l, :].partition_broadcast(128), reads=[I["s_d"]], writes=[dsk])
            t2p = self.tpool("t2", [128, 512], F32, 2)
            for i in tiles:
                ex = exp_.next(); t2 = t2p.next()
                P.dma("sync", ex[:], S["sz"].t[i * 128:(i + 1) * 128, :], reads=[S["sz"]], writes=[ex])
                P.op("vector", lambda e, t2=t2, i=i: e.tensor_tensor(t2[:], V[:, i, :, :].rearrange("p h d -> p (h d)"), dsk[:], ALU.mult), reads=[V, dsk], writes=[t2])
                P.op("gpsimd", lambda e, t2=t2, i=i: e.tensor_tensor(t2[:], t2[:], yacc[:, i, :], ALU.add), reads=[t2, yacc], writes=[t2])
                P.op("vector", lambda e, t2=t2, ex=ex: e.tensor_tensor(t2[:], t2[:], ex[:], ALU.mult), reads=[t2, ex], writes=[t2])
                self.finalize(t2[:], t2, 2, gain, None, 2, i)

    def stage_attn(self, l, bi):
        P, I, S = self.P, self.I, self.S
        self.reset_arena()
        emit = l < DEPTH - 1
        lam_init = 0.8 - 0.6 * math.exp(-0.3 * l)
        qT = self.tile("qT", [128, 4, T], BF16)
        kT = self.tile("kT", [128, 4, T], BF16)
        V = self.tile("aV", [128, NT, 4, 130], BF16)
        ya = self.tile("ya", [128, NT, 512], F32)
        g64 = self.tile("g64", [128, 2, 64], F32)
        gq = self.tile("gq", [128, 8, 64], F32)
        gk = self.tile("gk", [128, 8, 64], F32)
        la = self.tile("la", [128, 256], F32)
        lt = self.tile("lt", [128, 128], F32)
        ls = self.tile("ls", [128, 4], F32)
        P.dma("sync", g64[:, 0, :], I["a_qnorm_g"].t[l, :].partition_broadcast(128), reads=[I["a_qnorm_g"]], writes=[g64])
        P.dma("sync", g64[:, 1, :], I["a_knorm_g"].t[l, :].partition_broadcast(128), reads=[I["a_knorm_g"]], writes=[g64])
        P.op("vector", lambda e: e.tensor_copy(gq[:], g64[:, 0:1, :].to_broadcast([128, 8, 64])), reads=[g64], writes=[gq])
        P.op("vector", lambda e: e.tensor_copy(gk[:], g64[:, 1:2, :].to_broadcast([128, 8, 64])), reads=[g64], writes=[gk])
        P.dma("sync", la[:], I["a_lambda"].t[l, :].partition_broadcast(128), reads=[I["a_lambda"]], writes=[la])
        P.op("vector", lambda e: e.tensor_tensor(lt[:, 0:64], la[:, 0:64], la[:, 64:128], ALU.mult), reads=[la], writes=[lt])
        P.op("vector", lambda e: e.tensor_tensor(lt[:, 64:128], la[:, 128:192], la[:, 192:256], ALU.mult), reads=[la, lt], writes=[lt])
        P.op("vector", lambda e: e.tensor_reduce(ls[:, 0:2], lt[:].rearrange("p (a d) -> p a d", a=2), axis=AX.X, op=ALU.add), reads=[lt], writes=[ls])
        P.op("scalar", lambda e: e.activation(ls[:, 0:2], ls[:, 0:2], AF.Exp), reads=[ls], writes=[ls])
        P.op("vector", lambda e: e.tensor_tensor(ls[:, 2:3], ls[:, 1:2], ls[:, 0:1], ALU.subtract), reads=[ls], writes=[ls])
        P.op("vector", lambda e: e.tensor_scalar(ls[:, 3:4], ls[:, 2:3], -lam_init, None, ALU.add), reads=[ls], writes=[ls])
        P.op("vector", lambda e: e.memset(V[:, :, :, 128:130], 1.0), writes=[V])
        xp = self.tpool("ax", [128, 512], F32, 2)
        sqp = self.tpool("asq", [128, 512], F32, 2)
        xnp = self.tpool("axn", [128, 512], F32, 2)
        xrp = self.tpool("axr", [128, 512], BF16, 2)
        t1p = self.tpool("at1", [128, 256], F32, 2)
        t2p = self.tpool("at2", [128, 256], F32, 2)
        stp = self.tpool("ast", [128, 16], F32, 2)
        csp = self.tpool("acs", [128, 2, 32], F32, 2)

        def prep(i, src, dstT, g, cs):
            x = xp.next(); q = sqp.next(); xn = xnp.next(); xr = xrp.next(); st = stp.next()
            P.dma("sync", x[:], src.t[i * 128:(i + 1) * 128, :], reads=[src], writes=[x])
            P.op("scalar", lambda e: e.activation(q[:], x[:], AF.Square), reads=[x], writes=[q])
            P.op("vector", lambda e: e.tensor_reduce(st[:, 0:8], q[:].rearrange("p (g d) -> p g d", g=8), axis=AX.X, op=ALU.add), reads=[q], writes=[st])
            P.op("scalar", lambda e: e.activation(st[:, 8:16], st[:, 0:8], AF.Sqrt, bias=1e-6, scale=1.0 / 64), reads=[st], writes=[st])
            P.op("vector", lambda e: e.reciprocal(st[:, 8:16], st[:, 8:16]), reads=[st], writes=[st])
            P.op("vector", lambda e: e.tensor_tensor(xn[:].rearrange("p (g d) -> p g d", g=8), x[:].rearrange("p (g d) -> p g d", g=8),
                                                     st[:, 8:16].unsqueeze(2).to_broadcast([128, 8, 64]), ALU.mult), reads=[x, st], writes=[xn])
            if cs is None:
                P.op("gpsimd", lambda e: e.tensor_tensor(xr[:], xn[:], g[:].rearrange("p g d -> p (g d)"), ALU.mult), reads=[xn, g], writes=[xr])
            else:
                P.op("gpsimd", lambda e: e.tensor_tensor(xn[:], xn[:], g[:].rearrange("p g d -> p (g d)"), ALU.mult), reads=[xn, g], writes=[xn])
                t1 = t1p.next(); t2 = t2p.next()
                xv = xn[:].rearrange("p (g a h f) -> p g a h f", g=8, a=2, h=2)
                xo = xr[:].rearrange("p (g a h f) -> p g a h f", g=8, a=2, h=2)
                x1, x2 = xv[:, :, :, 0, :], xv[:, :, :, 1, :]
                cv = cs[:, 0, :].rearrange("p (a f) -> p a f", a=2).unsqueeze(1).to_broadcast([128, 8, 2, 16])
                sv = cs[:, 1, :].rearrange("p (a f) -> p a f", a=2).unsqueeze(1).to_broadcast([128, 8, 2, 16])
                t1v = t1[:].rearrange("p (g a f) -> p g a f", g=8, a=2)
                t2v = t2[:].rearrange("p (g a f) -> p g a f", g=8, a=2)
                P.op("vector", lambda e: e.tensor_tensor(t1v, x1, cv, ALU.mult), reads=[xn, cs], writes=[t1])
                P.op("gpsimd", lambda e: e.tensor_tensor(t2v, x2, sv, ALU.mult), reads=[xn, cs], writes=[t2])
                P.op("vector", lambda e: e.tensor_tensor(xo[:, :, :, 0, :], t1v, t2v, ALU.subtract), reads=[t1, t2], writes=[xr])
                P.op("vector", lambda e: e.tensor_tensor(t1v, x2, cv, ALU.mult), reads=[xn, cs, xr], writes=[t1])
                P.op("gpsimd", lambda e: e.tensor_tensor(t2v, x1, sv, ALU.mult), reads=[xn, cs, xr], writes=[t2])
                P.op("vector", lambda e: e.tensor_tensor(xo[:, :, :, 1, :], t1v, t2v, ALU.add), reads=[t1, t2], writes=[xr])
            ps = self.pt.next()
            for h in range(4):
                P.op("tensor", lambda e, h=h: e.transpose(ps[:, h, :], xr[:, h * 128:(h + 1) * 128], self.ident_b[:]), reads=[xr, self.ident_b], writes=[ps])
            P.op("scalar", lambda e: e.copy(dstT[:, :, i * 128:(i + 1) * 128], ps[:, 0:4, :]), reads=[ps], writes=[dstT])

        for i in range(NT):
            cs = None
            if i >= 2:
                cs = csp.next()
                P.dma("sync", cs[:, 0, :], I["k_cos"].t[(i - 2) * 128:(i - 1) * 128, :], reads=[I["k_cos"]], writes=[cs])
                P.dma("sync", cs[:, 1, :], I["k_sin"].t[(i - 2) * 128:(i - 1) * 128, :], reads=[I["k_sin"]], writes=[cs])
            prep(i, S["aq"], qT, gq, cs)
            prep(i, S["ak"], kT, gk, cs)
            P.dma("sync", V[:, i, :, 0:128], S["av"].t[i * 128:(i + 1) * 128, :].rearrange("p (h d) -> p h d", h=4), reads=[S["av"]], writes=[V])

        PTp = self.tpool("aPT", [128, 512], BF16, 3)
        rp = self.tpool("ar", [128, 2], F32, 2)
        tmpp = self.tpool("atmp", [128, 128], F32, 2)
        px = self.px
        O0p = self.tpool("aO0", [128, 4, 129], F32, 2)

        def qblock(h, q0, qw):
            kts = list(range(NT)) if q0 >= TC else [0, 1]
            nq = qw // 128
            O0 = O0p.next()
            for j in range(2):
                for kt in kts:
                    score(h, q0, qw, j, kt, kt == kts[0], kt == kts[-1], nq)
                if j == 0:
                    for qi in range(nq):
                        evac0(O0, qi)
            for qi in range(nq):
                combine(h, (q0 + qi * 128) // 128, qi, O0)

        def evac0(O0, qi):
            P.op("scalar", lambda e: e.copy(O0[:, qi, :], px[qi][:, 0:129]), reads=[px[qi]], writes=[O0])

        def score(h, q0, qw, j, kt, st_, sp_, nq):
            sp = self.pm.next()
            pb = PTp.next()
            P.op("tensor", lambda e: e.matmul(sp[:, 0:qw], kT[64 * j:64 * j + 64, h, kt * 128:(kt + 1) * 128], qT[64 * j:64 * j + 64, h, q0:q0 + qw], start=True, stop=True), reads=[kT, qT], writes=[sp])
            P.op("scalar", lambda e: e.activation(pb[:, 0:qw], sp[:, 0:qw], AF.Exp, scale=0.125), reads=[sp], writes=[pb])
            for qi in range(nq):
                P.op("tensor", lambda e, qi=qi: e.matmul(px[qi][:, 0:129], pb[:, qi * 128:(qi + 1) * 128], V[:, kt, h, 0:129], start=st_, stop=sp_), reads=[pb, V], writes=[px[qi]])

        def combine(h, ti, qi, O0):
            o1 = px[qi]
            r = rp.next(); tmp = tmpp.next()
            P.op("vector", lambda e: e.reciprocal(r[:, 0:1], O0[:, qi, 128:129]), reads=[O0], writes=[r])
            P.op("vector", lambda e: e.reciprocal(r[:, 1:2], o1[:, 128:129]), reads=[o1, r], writes=[r])
            P.op("vector", lambda e: e.tensor_tensor(r[:, 1:2], r[:, 1:2], ls[:, 3:4], ALU.mult), reads=[r, ls], writes=[r])
            P.op("vector", lambda e: e.tensor_scalar(tmp[:], O0[:, qi, 0:128], r[:, 0:1], None, ALU.mult), reads=[O0, r], writes=[tmp])
            P.op("vector", lambda e: e.scalar_tensor_tensor(ya[:, ti, h * 128:(h + 1) * 128], o1[:, 0:128], r[:, 1:2], tmp[:], ALU.mult, ALU.add), reads=[o1, r, tmp], writes=[ya])

        qblocks = [(TC + 512 * k, 512) for k in range(4)] + ([(0, TC)] if emit else [])
        for h in range(4):
            for (q0, qw) in qblocks:
                qblock(h, q0, qw)

        self.fin_setup()
        gain = self.tile("again", [128, 4, 128], F32)
        for h in range(4):
            P.dma("sync", gain[:, h, :], I["a_subln_g"].t[l, :].partition_broadcast(128), reads=[I["a_subln_g"]], writes=[gain])
        P.op("vector", lambda e: e.tensor_scalar(gain[:], gain[:], 1.0 - lam_init, None, ALU.mult), reads=[gain], writes=[gain])
        gflat = Res("againf", gain[:].rearrange("p h d -> p (h d)"))
        gflat.w = gain.w
        for i in (range(NT) if emit else range(2, NT)):
            self.finalize(ya[:, i, :], ya, 4, gflat_sync(gflat, gain), None, 1, i)

    def stage_merge(self, l, bi):
        P, I, S = self.P, self.I, self.S
        self.reset_arena()
        last = l == DEPTH - 1
        self.wst = self.tpool("wst", [128, 8, 512], F32, 2)
        wbr = [self.tile("wbr%d" % k, [128, 4, D], BF16) for k in range(3)]
        wo = self.tile("wo", [128, 8, D], BF16)
        for k, nm in enumerate(("w_branch_m", "w_branch_a", "w_branch_s")):
            for c0 in (0, 512):
                st = self.wst.next()
                P.dma("sync", st[:, 0:4, :], I[nm].t[l, :, c0:c0 + 512].rearrange("(c p) n -> p c n", p=128), reads=[I[nm]], writes=[st])
                P.op("gpsimd", lambda e, st=st, k=k, c0=c0: e.tensor_copy(wbr[k][:, :, c0:c0 + 512], st[:, 0:4, :]), reads=[st], writes=[wbr[k]])
        for c0 in (0, 512):
            st = self.wst.next()
            P.dma("sync", st[:], I["w_out"].t[l, :, c0:c0 + 512].rearrange("(c p) n -> p c n", p=128), reads=[I["w_out"]], writes=[st])
            P.op("gpsimd", lambda e, st=st, c0=c0: e.tensor_copy(wo[:, :, c0:c0 + 512], st[:]), reads=[st], writes=[wo])
        gate = [self.tile("mgate%d" % v, [128, D], F32) for v in range(2)]
        for v, row in ((0, 4), (1, bi)):
            P.dma("sync", gate[v][:], S["modv"].t[l, row, 2 * D:3 * D].partition_broadcast(128), reads=[S["modv"]], writes=[gate[v]])
        yTp = self.tpool("myT", [128, 3, 4, 512], BF16, 2)
        bgp = self.tpool("mbg", [128, 24, 512], BF16, 2)
        mTp = self.tpool("mmT", [128, 8, 512], BF16, 2)
        m0p = self.tpool("mm0", [128, 512], F32, 2)
        m1p = self.tpool("mm1", [128, 512], F32, 2)
        xp = self.tpool("mx", [128, D], F32, 2)
        tp = self.tpool("mtp", [128, 512], F32, 2)
        px = self.px

        def cblock(c, tw, yT, bg, mT):
            pss = [px[0], px[1], px[2]]
            for k in range(3):
                for kc in range(4):
                    P.op("tensor", lambda e, k=k, kc=kc: e.matmul(pss[k][:, 0:tw], wbr[k][:, kc, c * 128:(c + 1) * 128], yT[:, k, kc, 0:tw], start=(kc == 0), stop=(kc == 3)), reads=[wbr[k], yT], writes=[pss[k]])
            m0 = m0p.next(); m1 = m1p.next()
            P.op("vector", lambda e: e.tensor_tensor(m0[:, 0:tw], pss[0][:, 0:tw], bg[:, c, 0:tw], ALU.mult), reads=[pss[0], bg], writes=[m0])
            P.op("vector", lambda e: e.tensor_tensor(m1[:, 0:tw], pss[1][:, 0:tw], bg[:, 8 + c, 0:tw], ALU.mult), reads=[pss[1], bg], writes=[m1])
            P.op("gpsimd", lambda e: e.tensor_tensor(m0[:, 0:tw], m0[:, 0:tw], m1[:, 0:tw], ALU.add), reads=[m0, m1], writes=[m0])
            P.op("vector", lambda e: e.tensor_tensor(m1[:, 0:tw], pss[2][:, 0:tw], bg[:, 16 + c, 0:tw], ALU.mult), reads=[pss[2], bg, m1], writes=[m1])
            P.op("gpsimd", lambda e: e.tensor_tensor(mT[:, c, 0:tw], m0[:, 0:tw], m1[:, 0:tw], ALU.add), reads=[m0, m1], writes=[mT])

        def outtile(ti, tl, mT):
            v = 0 if ti < 2 else 1
            x = xp.next()
            P.dma("sync", x[:], S["xres"].t[ti * 128:(ti + 1) * 128, :], reads=[S["xres"]], writes=[x])
            for nb_ in range(2):
                ps = self.pm.next()
                t = tp.next()
                for kc in range(8):
                    P.op("tensor", lambda e, kc=kc, ps=ps, nb_=nb_: e.matmul(ps[:], mT[:, kc, tl * 128:(tl + 1) * 128], wo[:, kc, nb_ * 512:(nb_ + 1) * 512], start=(kc == 0), stop=(kc == 7)), reads=[mT, wo], writes=[ps])
                P.op("vector", lambda e, ps=ps, t=t, nb_=nb_: e.tensor_tensor(t[:], ps[:], gate[v][:, nb_ * 512:(nb_ + 1) * 512], ALU.mult), reads=[ps, gate[v]], writes=[t])
                P.op("gpsimd", lambda e, t=t, nb_=nb_: e.tensor_tensor(x[:, nb_ * 512:(nb_ + 1) * 512], x[:, nb_ * 512:(nb_ + 1) * 512], t[:], ALU.add), reads=[t, x], writes=[x])
            P.dma("scalar", S["xres"].t[ti * 128:(ti + 1) * 128, :], x[:], reads=[x], writes=[S["xres"]])

        for (t0, tw) in (TB[1:] if last else TB):
            yT = yTp.next(); bg = bgp.next(); mT = mTp.next()
            P.dma("sync", yT[:, :, :, 0:tw], S["yT"].t[:, :, t0:t0 + tw].rearrange("k (c p) t -> p k c t", p=128), reads=[S["yT"]], writes=[yT])
            P.dma("sync", bg[:, :, 0:tw], S["bgT"].t[:, t0:t0 + tw].rearrange("(k p) t -> p k t", p=128), reads=[S["bgT"]], writes=[bg])
            for c in range(8):
                cblock(c, tw, yT, bg, mT)
            for tl in range(tw // 128):
                outtile(t0 // 128 + tl, tl, mT)

    def stage_ffn(self, l, bi, out_ap=None):
        P, I, S = self.P, self.I, self.S
        self.reset_arena()
        last = l == DEPTH - 1
        hT = self.tile("hT", [128, 8, T], BF16)
        self.wst = self.tpool("wst", [128, 8, 512], F32, 2)
        self.wbf = self.tpool("wbf", [128, 8, 512], BF16, 2)
        mark = self.aoff
        tiles = range(2, NT) if last else range(NT)
        self.norm_stage(l, bi, "norm2_g", 3, 4, hT, tiles)
        self.P.barrier()
        self.aoff = mark
        tbs = TB[1:] if last else TB
        segs = SEGS[1:] if last else SEGS
        tlo = TC if last else 0
        cwf = self.tile("cwf", [128, 44, 3], F32)
        P.dma("sync", cwf[:], I["f_cw"].t[l], reads=[I["f_cw"]], writes=[cwf])
        rba = self.tpool("rba", [128, T], F32, 2)
        rbg = self.tpool("rbg", [128, T], F32, 2)
        acp = self.tpool("fac", [128, T], F32, 2)
        gcp = self.tpool("fgc", [128, T], F32, 2)
        obp = self.tpool("fob", [128, T], BF16, 2)
        W = I["w_up"]

        def pair(f):
            ra = rba.next(); rg = rbg.next(); ac = acp.next(); gc = gcp.next(); ob = obp.next()
            for (rb, col) in ((ra, f * 128), (rg, D_FF + f * 128)):
                wb = self.load_w(W.t[l, :, col:col + 128], 8, 128, W)
                for (t0, tw) in tbs:
                    ps = self.pm.next()
                    for k in range(8):
                        P.op("tensor", lambda e, ps=ps, wb=wb, k=k, t0=t0, tw=tw: e.matmul(ps[:, 0:tw], wb[:, k, 0:128], hT[:, k, t0:t0 + tw], start=(k == 0), stop=(k == 7)), reads=[hT, wb], writes=[ps])
                    P.op("scalar", lambda e, ps=ps, rb=rb, t0=t0, tw=tw: e.copy(rb[:, t0:t0 + tw], ps[:, 0:tw]), reads=[ps], writes=[rb])
            self.conv_rows(ra, ac, cwf, f, False, segs)
            self.conv_rows(rg, gc, cwf, 22 + f, False, segs)
            P.op("scalar", lambda e: e.activation(gc[:, tlo:T], gc[:, tlo:T], AF.Silu), reads=[gc], writes=[gc])
            P.op("gpsimd", lambda e: e.tensor_tensor(ob[:, tlo:T], gc[:, tlo:T], ac[:, tlo:T], ALU.mult), reads=[gc, ac], writes=[ob])
            P.dma("scalar", S["actT"].t[f * 128:(f + 1) * 128, tlo:T], ob[:, tlo:T], reads=[ob], writes=[S["actT"]])

        for f in range(22):
            pair(f)

        self.reset_arena()
        self.wst = self.tpool("wst", [128, 8, 512], F32, 2)
        wd = self.tile("wd", [128, 22, D], BF16)
        for c0 in (0, 512):
            for k0 in (0, 8, 16):
                kn = min(8, 22 - k0)
                st = self.wst.next()
                P.dma("sync", st[:, 0:kn, :], I["w_down"].t[l, k0 * 128:(k0 + kn) * 128, c0:c0 + 512].rearrange("(c p) n -> p c n", p=128), reads=[I["w_down"]], writes=[st])
                P.op("gpsimd", lambda e, st=st, k0=k0, kn=kn, c0=c0: e.tensor_copy(wd[:, k0:k0 + kn, c0:c0 + 512], st[:, 0:kn, :]), reads=[st], writes=[wd])
        gate = [self.tile("fgate%d" % v, [128, D], F32) for v in range(2)]
        for v, row in ((0, 4), (1, bi)):
            P.dma("sync", gate[v][:], S["modv"].t[l, row, 5 * D:6 * D].partition_broadcast(128), reads=[S["modv"]], writes=[gate[v]])
        aTp = self.tpool("faT", [128, 22, 128], BF16, 3)
        xp = self.tpool("fx", [128, D], F32, 2)
        tp = self.tpool("ftp", [128, 512], F32, 2)

        def dtile(ti):
            v = 0 if ti < 2 else 1
            aT = aTp.next(); x = xp.next()
            P.dma("sync", aT[:], S["actT"].t[:, ti * 128:(ti + 1) * 128].rearrange("(c p) t -> p c t", p=128), reads=[S["actT"]], writes=[aT])
            P.dma("sync", x[:], S["xres"].t[ti * 128:(ti + 1) * 128, :], reads=[S["xres"]], writes=[x])
            for nb_ in range(2):
                ps = self.pm.next()
                t = tp.next()
                for kc in range(22):
                    P.op("tensor", lambda e, kc=kc, ps=ps, nb_=nb_: e.matmul(ps[:], aT[:, kc, :], wd[:, kc, nb_ * 512:(nb_ + 1) * 512], start=(kc == 0), stop=(kc == 21)), reads=[aT, wd], writes=[ps])
                P.op("vector", lambda e, ps=ps, t=t, nb_=nb_: e.tensor_tensor(t[:], ps[:], gate[v][:, nb_ * 512:(nb_ + 1) * 512], ALU.mult), reads=[ps, gate[v]], writes=[t])
                P.op("gpsimd", lambda e, t=t, nb_=nb_: e.tensor_tensor(x[:, nb_ * 512:(nb_ + 1) * 512], x[:, nb_ * 512:(nb_ + 1) * 512], t[:], ALU.add), reads=[t, x], writes=[x])
            if last:
                P.dma("scalar", self.out.t[bi, (ti - 2) * 128:(ti - 1) * 128, :], x[:], reads=[x], writes=[self.out])
            else:
                P.dma("scalar", S["xres"].t[ti * 128:(ti + 1) * 128, :], x[:], reads=[x], writes=[S["xres"]])

        for ti in tiles:
            dtile(ti)

    def build(self):
        self.declare()
        self.consts()
        if self.only is not None:
            for st in self.only:
                st(self)
            return self.finish()
        self.stage_mod()
        if self.stop_after == "mod":
            return self.finish()
        P, I, S = self.P, self.I, self.S
        for bi in range(self.nb):
            self.P.barrier()
            P.dma("sync", S["xres"].t[0:TC, :], I["ctx"].t[bi], reads=[I["ctx"]], writes=[S["xres"]])
            P.dma("sync", S["xres"].t[TC:T, :], I["x"].t[bi], reads=[I["x"]], writes=[S["xres"]])
            for l in self.layers:
                self.P.new_epoch()
                self.stage_inproj(l, bi)
                if self.stop_after == "inproj":
                    return self.finish()
                self.stage_scan(l, bi, "m")
                self.stage_attn(l, bi)
                self.stage_scan(l, bi, "s")
                self.stage_merge(l, bi)
                self.stage_ffn(l, bi)
        return self.finish()

    def finish(self):
        self.P.barrier()
        self.P.emit()
        self.P.close()
        return self.nc


def gflat_sync(gflat, gain):
    gflat.w = dict(gain.w)
    return gflat


def host_consts():
    k = {}
    k["k_ident"] = np.eye(128, dtype=np.float32)
    r = np.arange(128)
    tri = np.stack([(r[:, None] <= r[None, :]), (r[:, None] >= r[None, :])]).astype(np.float32)
    k["k_tri"] = tri
    neg = np.where(tri > 0, 0.0, -30000.0).astype(np.float32)
    k["k_neg"] = np.ascontiguousarray(np.tile(neg, (1, 1, 4)))
    rows = TL // 64
    row = np.repeat(np.arange(rows), 64)
    col = np.tile(np.arange(64), rows)
    inv = (10000.0 ** (-np.arange(16, dtype=np.float32) / 16)).astype(np.float32)
    ang = np.stack([row, col], -1).astype(np.float32)[..., None] * inv
    k["k_cos"] = np.cos(ang).astype(np.float32).reshape(TL, 32)
    k["k_sin"] = np.sin(ang).astype(np.float32).reshape(TL, 32)
    return k


def host_layout(inp, cores, nb, ag=USE_AG):
    f = lambda a: np.ascontiguousarray(np.asarray(a, dtype=np.float32))
    sh = {}
    sh["b_mod"] = f(inp["b_mod"])
    sh["norm1_g"] = f(inp["norm1_g"]); sh["norm2_g"] = f(inp["norm2_g"])
    wall = None
    if ag:
        wall = np.concatenate([np.asarray(inp[nm], np.float32)[l].reshape(-1) for l in range(DEPTH) for nm, _, _ in WSPEC]).reshape(NCORE, WROWS, WCOLS)
    else:
        for nm, _, _ in WSPEC:
            sh[nm] = f(inp[nm])

    def cwl(w, b, nblk):
        parts = [np.asarray(w, np.float32)] + ([np.asarray(b, np.float32)[:, None, :]] if b is not None else [])
        a = np.concatenate(parts, axis=1)
        L, kk, C = a.shape
        return np.ascontiguousarray(a.reshape(L, kk, nblk, 128).transpose(0, 3, 2, 1))

    sh["m_cw"] = cwl(inp["m_conv_w"], inp["m_conv_b"], 8)
    sh["s_cw"] = cwl(inp["s_conv_w"], inp["s_conv_b"], 8)
    sh["f_cw"] = cwl(inp["ffn_conv_w"], None, 44)
    ig = np.asarray(inp["m_igate_b"], np.float32); fg = np.asarray(inp["m_fgate_b"], np.float32)
    sh["m_gb"] = np.ascontiguousarray(np.stack([ig, fg], axis=2).reshape(DEPTH, 16))
    sh["m_norm_g"] = f(inp["m_norm_g"]); sh["a_qnorm_g"] = f(inp["a_qnorm_g"]); sh["a_knorm_g"] = f(inp["a_knorm_g"])
    sh["a_lambda"] = f(inp["a_lambda"]).reshape(DEPTH, 256); sh["a_subln_g"] = f(inp["a_subln_g"])
    sh["s_dt_bias"] = f(inp["s_dt_bias"]).reshape(DEPTH, 16); sh["s_a_log"] = f(inp["s_a_log"]).reshape(DEPTH, 16)
    sh["s_d"] = np.ascontiguousarray(np.repeat(f(inp["s_d"]), 64, axis=1)); sh["s_norm_g"] = f(inp["s_norm_g"])
    sh.update(host_consts())
    x = np.asarray(inp["x"], np.float32); ctx = np.asarray(inp["ctx"], np.float32)
    c = np.asarray(inp["c"], np.float32); cc = np.asarray(inp["c_ctx"], np.float32)
    maps = []
    for k in cores:
        m = dict(sh)
        b0 = k * NB
        if ag:
            m["wsh"] = np.ascontiguousarray(wall[k])
        m["x"] = np.ascontiguousarray(x[b0:b0 + nb]); m["ctx"] = np.ascontiguousarray(ctx[b0:b0 + nb])
        call = np.concatenate([c[b0:b0 + NB], cc[None, :]], axis=0)
        m["cT"] = np.ascontiguousarray(call.reshape(5, 8, 128).transpose(2, 1, 0))
        maps.append(m)
    return maps


def kernel(**inputs):
    nc = Builder().build()
    maps = host_layout(inputs, list(range(NCORE)), NB)
    res = run_bass_kernel_spmd(nc, maps, core_ids=list(range(NCORE)))
    return np.concatenate([r["y"] for r in res.results], axis=0).astype(np.float32)
```
